# Optimizing a Trainium2 kernel written in Bass

```python
import jax, jax.numpy as jnp
from jax import lax
import numpy as np

D_MODEL = 1024
BATCH = 16
SEQ = 2048
DEPTH = 1

CTX_LEN = 256
GRID_W = 64
RET_HEADS = 4
RET_DIM = 128
RET_WIDTH = RET_HEADS * RET_DIM
RET_CHUNK = 128
NA_HEADS = 8
NA_DIM = 64
NA_WIDTH = NA_HEADS * NA_DIM
NA_KH = 8
NA_KW = 16
MIX_WIDTH = RET_WIDTH + NA_WIDTH
IN_SPLITS = (RET_WIDTH, RET_WIDTH, RET_WIDTH, RET_WIDTH, NA_WIDTH, NA_WIDTH, NA_WIDTH)
IN_WIDTH = 4 * RET_WIDTH + 3 * NA_WIDTH
D_FF = 4 * D_MODEL
ROPE_BASE = 10000.0
NORM_EPS = 1e-6
N_MOD = 6
NEG_INF = -1e30

kernel_name = 'hybrid_retention_natten_dit_block'


def rmsnorm(x, g):
    xf = x.astype(jnp.float32)
    y = xf * lax.rsqrt(jnp.mean(xf * xf, axis=-1, keepdims=True) + NORM_EPS)
    return (y * g.astype(jnp.float32)).astype(x.dtype)


def modulations(cvec, w_ada, b_ada):
    return jnp.split(jax.nn.silu(cvec) @ w_ada + b_ada, N_MOD, axis=-1)


def to_heads(t, n_heads):
    b, l, _ = t.shape
    return t.reshape(b, l, n_heads, -1).transpose(0, 2, 1, 3)


def axial_rope(x, pos_r, pos_c):
    d_axis = x.shape[-1] // 2
    n_freq = d_axis // 2
    inv = ROPE_BASE ** (-jnp.arange(n_freq, dtype=jnp.float32) / n_freq)

    def rot(seg, pos):
        ang = pos[:, None] * inv[None, :]
        cos, sin = jnp.cos(ang), jnp.sin(ang)
        s1 = seg[..., :n_freq].astype(jnp.float32)
        s2 = seg[..., n_freq:].astype(jnp.float32)
        return jnp.concatenate([s1 * cos - s2 * sin, s1 * sin + s2 * cos], axis=-1)

    out = jnp.concatenate([rot(x[..., :d_axis], pos_r), rot(x[..., d_axis:], pos_c)], axis=-1)
    return out.astype(x.dtype)


def retention_scan(q, k, v, log_gamma, state0, strict):
    b, h, l, dk = q.shape
    dv = v.shape[-1]
    nc = l // RET_CHUNK
    qc = (q * dk ** -0.5).reshape(b, h, nc, RET_CHUNK, dk)
    kc = k.reshape(b, h, nc, RET_CHUNK, dk)
    vc = v.reshape(b, h, nc, RET_CHUNK, dv)
    lg = log_gamma.astype(jnp.float32)[:, None]
    i = jnp.arange(RET_CHUNK, dtype=jnp.float32)
    diff = i[:, None] - i[None, :]
    mask = (diff > 0) if strict else (diff >= 0)
    decay = jnp.where(mask, jnp.exp(lg[:, :, None] * jnp.where(mask, diff, 0.0)), 0.0)
    scores = jnp.einsum('bhnid,bhnjd->bhnij', qc, kc) * decay[:, None]
    inner = jnp.einsum('bhnij,bhnje->bhnie', scores, vc)
    k_w = kc * jnp.exp(lg * (RET_CHUNK - 1.0 - i))[:, None, :, None]
    upd = jnp.einsum('bhnjd,bhnje->nbhde', k_w, vc).astype(jnp.float32)
    chunk_decay = jnp.exp(lg * RET_CHUNK)[:, :, None]

    def step(state, u):
        return chunk_decay * state + u, state

    state_final, state_prev = lax.scan(step, state0.astype(jnp.float32), upd)
    q_w = qc * jnp.exp(lg * (i + 1.0))[:, None, :, None]
    cross = jnp.einsum('bhnid,nbhde->bhnie', q_w, state_prev)
    out = (inner + cross).reshape(b, h, l, dv).astype(v.dtype)
    return out, state_final


def head_layernorm(o, w):
    of = o.astype(jnp.float32)
    mu = jnp.mean(of, axis=-1, keepdims=True)
    var = jnp.mean(jnp.square(of - mu), axis=-1, keepdims=True)
    y = (of - mu) * lax.rsqrt(var + NORM_EPS)
    b, h, l, dv = o.shape
    y = y.transpose(0, 2, 1, 3).reshape(b, l, h * dv)
    return (y * w.astype(jnp.float32)).astype(o.dtype)


def retention_mixer(q, k, v, g, qc, kc, vc, gc, log_gammas, gn_w, with_ctx_out):
    n = q.shape[1]
    tok = jnp.arange(n)
    pos_r = (tok // GRID_W).astype(jnp.float32)
    pos_c = (tok % GRID_W).astype(jnp.float32)
    q = axial_rope(to_heads(q, RET_HEADS), pos_r, pos_c)
    k = axial_rope(to_heads(k, RET_HEADS), pos_r, pos_c)
    v = to_heads(v, RET_HEADS)
    qc, kc, vc = to_heads(qc, RET_HEADS), to_heads(kc, RET_HEADS), to_heads(vc, RET_HEADS)
    b = q.shape[0]
    zero = jnp.zeros((b, RET_HEADS, RET_DIM, RET_DIM), jnp.float32)

    def flip(t):
        return jnp.flip(t, axis=2)

    ctx_f, s_f = retention_scan(qc, kc, vc, log_gammas[0], zero, False)
    lat_f, _ = retention_scan(q, k, v, log_gammas[0], s_f, False)
    ctx_b, s_b = retention_scan(flip(qc), flip(kc), flip(vc), log_gammas[1], zero, True)
    lat_b, _ = retention_scan(flip(q), flip(k), flip(v), log_gammas[1], s_b, True)
    lat = head_layernorm(lat_f + flip(lat_b), gn_w) * jax.nn.silu(g)
    ctx_out = None
    if with_ctx_out:
        ctx_out = head_layernorm(ctx_f + flip(ctx_b), gn_w) * jax.nn.silu(gc)
    return lat, ctx_out


def neighbourhood_attention(q, k, v, kc, vc, rpb):
    b, n, _ = q.shape
    rows = n // GRID_W
    kh = min(NA_KH, rows)

    def grid(t):
        return t.reshape(b, rows, GRID_W, NA_HEADS, NA_DIM).transpose(0, 3, 1, 2, 4)

    qg = grid(q) * NA_DIM ** -0.5
    kg, vg = grid(k), grid(v)
    kc, vc = to_heads(kc, NA_HEADS), to_heads(vc, NA_HEADS)
    r = jnp.arange(rows)
    row_idx = jnp.clip(r - kh // 2, 0, rows - kh)[:, None] + jnp.arange(kh)[None, :]
    nk = kh * GRID_W
    k_band = kg[:, :, row_idx].reshape(b, NA_HEADS, rows, nk, NA_DIM)
    v_band = vg[:, :, row_idx].reshape(b, NA_HEADS, rows, nk, NA_DIM)
    col = jnp.arange(GRID_W)
    col_start = jnp.clip(col - NA_KW // 2, 0, GRID_W - NA_KW)
    key_col = jnp.tile(col, kh)
    key_row = jnp.repeat(row_idx, GRID_W, axis=1)
    valid = (key_col[None, :] >= col_start[:, None]) & (key_col[None, :] < col_start[:, None] + NA_KW)
    dr = key_row - r[:, None] + (NA_KH - 1)
    dc = jnp.clip(key_col[None, :] - col[:, None] + (NA_KW - 1), 0, 2 * NA_KW - 2)
    bias = rpb[:, dr[:, None, :], dc[None, :, :]].astype(jnp.float32)
    bias = jnp.where(valid[None, None], bias, NEG_INF)
    s_loc = jnp.einsum('bhrqd,bhrkd->bhrqk', qg, k_band).astype(jnp.float32) + bias
    s_ctx = jnp.einsum('bhrqd,bhkd->bhrqk', qg, kc).astype(jnp.float32)
    p = jax.nn.softmax(jnp.concatenate([s_loc, s_ctx], axis=-1), axis=-1).astype(v.dtype)
    out = (jnp.einsum('bhrqk,bhrkd->bhrqd', p[..., :nk], v_band)
           + jnp.einsum('bhrqk,bhkd->bhrqd', p[..., nk:], vc))
    return out.transpose(0, 2, 3, 1, 4).reshape(b, n, NA_WIDTH)


def context_attention(qc, kc, vc):
    b, l, _ = qc.shape
    q, k, v = to_heads(qc, NA_HEADS), to_heads(kc, NA_HEADS), to_heads(vc, NA_HEADS)
    s = jnp.einsum('bhqd,bhkd->bhqk', q * NA_DIM ** -0.5, k).astype(jnp.float32)
    p = jax.nn.softmax(s, axis=-1).astype(v.dtype)
    o = jnp.einsum('bhqk,bhkd->bhqd', p, v)
    return o.transpose(0, 2, 1, 3).reshape(b, l, NA_WIDTH)


def squared_relu_mlp(h, w1, w2):
    return jnp.square(jax.nn.relu(h @ w1)) @ w2


def hybrid_layer(x, ctx, c, c_ctx, w_ada, b_ada, g_pre_mix, g_post_mix, g_pre_mlp, g_post_mlp,
                 w_in, ret_decay, ret_gn, na_rpb, w_out, w_mlp1, w_mlp2, update_ctx):
    sh1, sc1, gt1, sh2, sc2, gt2 = modulations(c[:, None, :], w_ada, b_ada)
    csh1, csc1, cgt1, csh2, csc2, cgt2 = modulations(c_ctx, w_ada, b_ada)
    split_at = [int(s) for s in np.cumsum(IN_SPLITS)[:-1]]
    h = rmsnorm(x, g_pre_mix) * (1.0 + sc1) + sh1
    hc = rmsnorm(ctx, g_pre_mix) * (1.0 + csc1) + csh1
    rq, rk, rv, rg, nq, nk, nv = jnp.split(h @ w_in, split_at, axis=-1)
    crq, crk, crv, crg, cnq, cnk, cnv = jnp.split(hc @ w_in, split_at, axis=-1)
    log_gammas = jax.nn.log_sigmoid(ret_decay.astype(jnp.float32))
    ret_lat, ret_ctx = retention_mixer(rq, rk, rv, rg, crq, crk, crv, crg, log_gammas, ret_gn, update_ctx)
    na_lat = neighbourhood_attention(nq, nk, nv, cnk, cnv, na_rpb)
    mix = jnp.concatenate([ret_lat, na_lat], axis=-1) @ w_out
    x = x + gt1 * rmsnorm(mix, g_post_mix)
    h2 = rmsnorm(x, g_pre_mlp) * (1.0 + sc2) + sh2
    x = x + gt2 * rmsnorm(squared_relu_mlp(h2, w_mlp1, w_mlp2), g_post_mlp)
    if update_ctx:
        na_ctx = context_attention(cnq, cnk, cnv)
        mix_c = jnp.concatenate([ret_ctx, na_ctx], axis=-1) @ w_out
        ctx = ctx + cgt1 * rmsnorm(mix_c, g_post_mix)
        hc2 = rmsnorm(ctx, g_pre_mlp) * (1.0 + csc2) + csh2
        ctx = ctx + cgt2 * rmsnorm(squared_relu_mlp(hc2, w_mlp1, w_mlp2), g_post_mlp)
    return x, ctx


def setup_inputs(seed: int = 0) -> dict:
    key = jax.random.key(seed)
    ks = jax.random.split(key, 17)

    def nrm(k, shape, s):
        return jax.random.normal(k, shape, jnp.float32) * s

    base_logit = jnp.log(2.0 ** (5.0 + jnp.arange(RET_HEADS, dtype=jnp.float32)) - 1.0)
    return {
        'x': nrm(ks[0], (BATCH, SEQ, D_MODEL), 1.0),
        'c': nrm(ks[1], (BATCH, D_MODEL), 1.0),
        'ctx': nrm(ks[2], (BATCH, CTX_LEN, D_MODEL), 1.0),
        'c_ctx': nrm(ks[3], (D_MODEL,), 1.0),
        'w_ada': nrm(ks[4], (DEPTH, D_MODEL, N_MOD * D_MODEL), D_MODEL ** -0.5),
        'b_ada': nrm(ks[5], (DEPTH, N_MOD * D_MODEL), 0.02),
        'g_pre_mix': 1.0 + nrm(ks[6], (DEPTH, D_MODEL), 0.02),
        'g_post_mix': 1.0 + nrm(ks[7], (DEPTH, D_MODEL), 0.02),
        'g_pre_mlp': 1.0 + nrm(ks[8], (DEPTH, D_MODEL), 0.02),
        'g_post_mlp': 1.0 + nrm(ks[9], (DEPTH, D_MODEL), 0.02),
        'w_in': nrm(ks[10], (DEPTH, D_MODEL, IN_WIDTH), D_MODEL ** -0.5),
        'ret_decay': base_logit[None, None, :] + nrm(ks[11], (DEPTH, 2, RET_HEADS), 0.1),
        'ret_gn': 1.0 + nrm(ks[12], (DEPTH, RET_WIDTH), 0.02),
        'na_rpb': nrm(ks[13], (DEPTH, NA_HEADS, 2 * NA_KH - 1, 2 * NA_KW - 1), 0.1),
        'w_out': nrm(ks[14], (DEPTH, MIX_WIDTH, D_MODEL), MIX_WIDTH ** -0.5),
        'w_mlp1': nrm(ks[15], (DEPTH, D_MODEL, D_FF), D_MODEL ** -0.5),
        'w_mlp2': nrm(ks[16], (DEPTH, D_FF, D_MODEL), D_FF ** -0.5),
    }


def reference(x, c, ctx, c_ctx, w_ada, b_ada, g_pre_mix, g_post_mix, g_pre_mlp, g_post_mlp,
              w_in, ret_decay, ret_gn, na_rpb, w_out, w_mlp1, w_mlp2):
    for layer in range(DEPTH):
        x, ctx = hybrid_layer(x, ctx, c, c_ctx, w_ada[layer], b_ada[layer], g_pre_mix[layer],
                              g_post_mix[layer], g_pre_mlp[layer], g_post_mlp[layer], w_in[layer],
                              ret_decay[layer], ret_gn[layer], na_rpb[layer], w_out[layer],
                              w_mlp1[layer], w_mlp2[layer], update_ctx=(layer + 1 < DEPTH))
    return x
```

```python
import contextlib
import numpy as np
import concourse.bass as bass
import concourse.mybir as mybir
from concourse.bass_utils import run_bass_kernel_spmd

F32 = mybir.dt.float32
BF16 = mybir.dt.bfloat16
AF = mybir.ActivationFunctionType
ALU = mybir.AluOpType

D = 1024
N = 2048
CTX = 256
NT = 16
TT = 18
EPS = 1e-6
NEG = -30000.0


class Buf:
    __slots__ = ("name", "w", "r")

    def __init__(self, name=""):
        self.name = name
        self.w = None
        self.r = []


def bufs(n, name=""):
    return [Buf(name + str(i)) for i in range(n)]


class Sched:
    def __init__(self, nc, sems):
        self.nc = nc
        self.eng = {"pe": nc.tensor, "act": nc.scalar, "dve": nc.vector, "pool": nc.gpsimd, "sp": nc.sync}
        self.sem = sems
        self.cnt = {k: 0 for k in sems}
        self.seen = {e: {k: 0 for k in sems} for e in self.eng}
        self.prog = {e: [] for e in self.eng}
        self.pending = {e: False for e in self.eng}

    def _deps(self, reads, writes):
        deps = {}
        for b in reads:
            if b.w is not None:
                k, v = b.w
                if deps.get(k, 0) < v:
                    deps[k] = v
        for b in writes:
            if b.w is not None:
                k, v = b.w
                if deps.get(k, 0) < v:
                    deps[k] = v
            for (k, v) in b.r:
                if deps.get(k, 0) < v:
                    deps[k] = v
        return deps

    def _wait(self, e, deps, skip=None):
        for k, v in deps.items():
            if k == skip:
                continue
            if k == e and (e == "pe" or v <= self.cnt[e] - 6):
                continue
            if self.seen[e][k] < v:
                self.prog[e].append(("w", k, v))
                self.seen[e][k] = v

    def _record(self, tok, reads, writes):
        for b in reads:
            b.r.append(tok)
            if len(b.r) > 24:
                m = {}
                for (k, v) in b.r:
                    if m.get(k, 0) < v:
                        m[k] = v
                b.r = list(m.items())
        for b in writes:
            b.w = tok
            b.r = []

    def op(self, e, fn, reads=(), writes=(), inc=True):
        self._wait(e, self._deps(reads, writes))
        if inc:
            self.cnt[e] += 1
            self.prog[e].append(("i", fn, e, 1))
            tok = (e, self.cnt[e])
            self.pending[e] = False
        else:
            self.prog[e].append(("i", fn, None, 0))
            tok = (e, self.cnt[e] + 1)
            self.pending[e] = True
        self._record(tok, reads, writes)
        return tok

    def dma(self, q, semname, fn, reads=(), writes=()):
        self._wait(q, self._deps(reads, writes), skip=semname)
        self.cnt[semname] += 16
        self.prog[q].append(("i", fn, semname, 16))
        tok = (semname, self.cnt[semname])
        self._record(tok, reads, writes)
        return tok

    def barrier(self):
        for e in self.eng:
            assert not self.pending[e]
        toks = {k: self.cnt[k] for k in self.sem}
        for e in self.eng:
            for k, v in toks.items():
                if k == e and e in ("pe", "sp"):
                    continue
                if v > 0 and self.seen[e][k] < v:
                    self.prog[e].append(("w", k, v))
                    self.seen[e][k] = v

    def final_wait(self, e="sp"):
        toks = {k: self.cnt[k] for k in self.sem}
        for k, v in toks.items():
            if v > 0 and self.seen[e][k] < v:
                self.prog[e].append(("w", k, v))
                self.seen[e][k] = v

    def emit(self):
        nc = self.nc

        def replay(key):
            def f(eng):
                for it in self.prog[key]:
                    if it[0] == "w":
                        eng.wait_ge(self.sem[it[1]], it[2])
                    else:
                        ins = it[1](eng)
                        if it[2] is not None:
                            ins.then_inc(self.sem[it[2]], it[3])
            return f

        with nc.Block() as block:
            block.tensor(replay("pe"))
            block.scalar(replay("act"))
            block.vector(replay("dve"))
            block.gpsimd(replay("pool"))
            block.sync(replay("sp"))


class Arena:
    def __init__(self, nc, base=16512, top=229344):
        self.nc = nc
        self.off = base
        self.top = top
        self.n = 0
        self.peak = base

    def alloc(self, shape, dt, name="t"):
        esz = 4 if dt == F32 else 2
        nbytes = int(np.prod(shape[1:])) * esz
        nbytes = (nbytes + 31) // 32 * 32
        off = self.off
        self.off += nbytes
        self.peak = max(self.peak, self.off)
        assert self.off <= self.top, ("SBUF overflow", name, self.off, self.top)
        self.n += 1
        h = self.nc.alloc_sbuf_tensor_at(f"{name}_{self.n}", list(shape), dt, offset=off)
        return h.ap()

    def mark(self):
        return self.off

    def reset(self, m):
        self.off = m


def na_key_tiles(t):
    if t <= 1:
        ul = [0, 1, 2, 3]
        idx0 = 5 + (0 - t + 3)
    elif t >= 14:
        ul = [12, 13, 14, 15]
        idx0 = 5 + (12 - t + 3)
    else:
        ul = [t - 2, t - 1, t, t + 1, t + 2]
        idx0 = 0
    return ul, idx0


def build(NB=2):
    nc = bass.Bass("TRN2", target_bir_lowering=False)

    def din(name, shape, dt=F32):
        return nc.dram_tensor(name, list(shape), dt, kind="ExternalInput").ap()

    x = din("x", [NB, N, D])
    ctx = din("ctx", [NB, CTX, D])
    cT_d = din("cT", [128, 24])
    w_ada = din("w_ada", [D, 6144])
    bada_d = din("b_ada3", [3, 6144])
    w_in = din("w_in", [D, 3584])
    w_out = din("w_out", [D, D])
    w1 = din("w1", [D, 4096])
    w2 = din("w2", [4096, D])
    gcols_d = din("gcols", [128, 16])
    grow_d = din("grow", [128, 2048 + 512 + 8])
    consts_d = din("consts", [128, 128 * 5 + 2])
    rope_d = din("rope", [128, 2 * 16 * 128])
    nab_d = din("nab", [8, 128, 12 * 128])
    sel_d = din("sel", [3, 3 * 128 + 3])
    y = nc.dram_tensor("y", [NB, N, D], F32, kind="ExternalOutput").ap()
    h2s = nc.dram_tensor("h2s", [NB, 128, 8, N], BF16, kind="ExternalOutput").ap()

    w_ada_v = w_ada.rearrange("(kc p) n -> p kc n", p=128)
    w_in_v = w_in.rearrange("(kc p) n -> p kc n", p=128)
    w_out_v = w_out.rearrange("(kc p) n -> p kc n", p=128)
    w1_v = w1.rearrange("(kc p) n -> p kc n", p=128)
    w2_v = w2.rearrange("(fc p) n -> p fc n", p=128)

    with contextlib.ExitStack() as st:
        semnames = ["pe", "act", "dve", "pool", "dx0", "dx1", "dwg0", "dwg1", "dnab0", "dnab1",
                    "dst0", "dst1", "dh0", "dh1", "dwo", "dw1", "dw2", "dmisc", "dxl0", "dxl1", "dwq0", "dwq1"]
        sems = {n_: st.enter_context(nc.semaphore(n_)) for n_ in semnames}
        S = Sched(nc, sems)
        A = Arena(nc)
        PB = [st.enter_context(nc.psum_tensor(f"ps{i}", [128, 512], F32)) for i in range(8)]
        PBb = [p[:].bitcast(BF16) for p in PB]
        Bp = [Buf(f"P{i}") for i in range(4)] + [None] * 4
        Bslot = {i: bufs(1, f"P{i}s") for i in range(4, 8)}

        def bank_bufs(i):
            return [Bp[i]] if i < 4 else Bslot[i]

        def MM(out, lhsT, rhs, st_, sp_, R, W, inc=True):
            return S.op("pe", lambda e: e.matmul(out, lhsT=lhsT, rhs=rhs, start=st_, stop=sp_), R, W, inc=inc)

        def TR(out, in_, R, W, inc=True):
            return S.op("pe", lambda e: e.transpose(out=out, in_=in_, identity=ident[:]), list(R) + [Bident], W, inc=inc)

        def ACT(out, in_, func, R, W, scale=None, bias=None, accum=None):
            kw = {}
            if scale is not None:
                kw["scale"] = scale
            if bias is not None:
                kw["bias"] = bias
            if accum is not None:
                kw["accum_out"] = accum
            return S.op("act", lambda e: e.activation(out=out, in_=in_, func=func, **kw), R, W)

        def TS(eng, out, in0, s1, s2, op0, op1, R, W):
            if op1 is None:
                return S.op(eng, lambda e: e.tensor_scalar(out=out, in0=in0, scalar1=s1, scalar2=None, op0=op0), R, W)
            return S.op(eng, lambda e: e.tensor_scalar(out=out, in0=in0, scalar1=s1, scalar2=s2, op0=op0, op1=op1), R, W)

        def TTo(eng, out, in0, in1, op, R, W):
            return S.op(eng, lambda e: e.tensor_tensor(out=out, in0=in0, in1=in1, op=op), R, W)

        def STT(out, in0, scalar, in1, op0, op1, R, W):
            return S.op("dve", lambda e: e.scalar_tensor_tensor(out=out, in0=in0, scalar=scalar, in1=in1, op0=op0, op1=op1), R, W)

        def CP(eng, out, in_, R, W):
            if eng == "act":
                return S.op("act", lambda e: e.activation(out=out, in_=in_, func=AF.Copy), R, W)
            return S.op(eng, lambda e: e.tensor_copy(out=out, in_=in_), R, W)

        def DMA(q, sem, out, in_, R, W):
            return S.dma(q, sem, lambda e: e.dma_start(out=out, in_=in_), R, W)

        def rstd_from(ms_ap, out_ap, Bms, Bout):
            return TTo("pool", out_ap, ms_ap, mhalf[:, 0:1], ALU.pow, [Bms, Bmh], [Bout])

        ident = A.alloc([128, 128], BF16, "ident")
        Bident = Buf("ident")
        mhalf = A.alloc([128, 8], F32, "mhalf")
        Bmh = Buf("mhalf")
        AB1a = A.alloc([128, 3, 8], F32, "A1")
        AB1b = A.alloc([128, 3, 8], F32, "B1")
        AB2a = A.alloc([128, 2, 8], F32, "A2")
        AB2b = A.alloc([128, 2, 8], F32, "B2")
        Bab = Buf("ab")
        GT1 = A.alloc([128, NB, 1024], F32, "GT1")
        GT2 = A.alloc([128, NB, 1024], F32, "GT2")
        Bgt = Buf("gt")
        stat = A.alloc([128, 64], F32, "stat")
        m_global = A.mark()

        cst = A.alloc([128, 128 * 5 + 2], F32, "cst")
        Bcst = Buf("cst")
        ropeT = A.alloc([128, 2, 16, 128], F32, "rope")
        Brope = Buf("rope")
        gnw = A.alloc([128, 512], F32, "gnw")
        Bgnw = Buf("gnw")
        lg = A.alloc([128, 8], F32, "lg")
        DT = A.alloc([128, 4, 128], F32, "DT")
        WF = A.alloc([128, 4, 128], F32, "WF")
        WB = A.alloc([128, 4, 128], F32, "WB")
        KW = A.alloc([128, 8], F32, "KW")
        GAM = A.alloc([128, 8], F32, "GAM")
        Bdec = Buf("dec")
        m_A = A.mark()

        cstf_ident = A.alloc([128, 128], F32, "identf")
        wadab = A.alloc([128, 8, 6144], BF16, "wada")
        Bwada = Buf("wada")
        cTf = A.alloc([128, 24], F32, "cTf")
        cTb = A.alloc([128, 8, 3], BF16, "cTb")
        BcT = Buf("cT")
        bada = A.alloc([3, 6144], F32, "bada")
        Bbada = Buf("bada")
        modsb = A.alloc([3, 6144], F32, "modsb")
        Bmod = Buf("mod")
        modT = A.alloc([128, 144], F32, "modT")
        BmodT = Buf("modT")
        gcols = A.alloc([128, 16], F32, "gcols")
        Bgc = Buf("gcols")
        grow = A.alloc([128, 2048 + 512 + 8], F32, "grow")
        Bgrow = Buf("grow")
        selsb = A.alloc([3, 3 * 128 + 3], F32, "sel")
        Bsel = Buf("sel")

        for i in range(4):
            DMA("pool", "dwo", wadab[:, :, i * 1536:(i + 1) * 1536], w_ada_v[:, :, i * 1536:(i + 1) * 1536], [], [Bwada])
        setup_bufs = [Bcst, Brope, BcT, Bbada, Bgc, Bgrow, Bsel]
        DMA("sp", "dmisc", cst[:], consts_d, [], [Bcst])
        DMA("sp", "dmisc", ropeT[:].rearrange("p a b c -> p (a b c)"), rope_d, [], [Brope])
        DMA("sp", "dmisc", cTf[:], cT_d, [], [BcT])
        DMA("sp", "dmisc", bada[:], bada_d, [], [Bbada])
        DMA("sp", "dmisc", gcols[:], gcols_d, [], [Bgc])
        DMA("sp", "dmisc", grow[:], grow_d, [], [Bgrow])
        DMA("sp", "dmisc", selsb[:], sel_d, [], [Bsel])
        for b_ in setup_bufs:
            b_.w = ("dmisc", S.cnt["dmisc"])

        S.op("dve", lambda e: e.memset(mhalf[:], -0.5), [], [Bmh])
        CP("dve", ident[:], cst[:, 0:128], [Bcst], [Bident])
        S.op("act", lambda e: e.activation(out=cTb[:].rearrange("p a b -> p (a b)"), in_=cTf[:], func=AF.Silu), [BcT], [BcT])
        for ns in range(12):
            pb = PB[ns % 2]
            for kc in range(8):
                MM(pb[0:3, :], cTb[:, kc, :], wadab[:, kc, ns * 512:(ns + 1) * 512], kc == 0, kc == 7,
                   [BcT, Bwada], [Bp[ns % 2]], inc=(kc == 7))
            TTo("dve", modsb[:, ns * 512:(ns + 1) * 512], pb[0:3, :], bada[:, ns * 512:(ns + 1) * 512], ALU.add,
                [Bp[ns % 2], Bbada], [Bmod])
        for c in range(48):
            MM(PB[2][:, c * 3:(c + 1) * 3], modsb[:, c * 128:(c + 1) * 128], selsb[:, 384:387], True, True,
               [Bmod, Bsel], [Bp[2]], inc=(c == 47))
        CP("dve", modT[:], PB[2][:, 0:144], [Bp[2]], [BmodT])
        modTv = modT[:].rearrange("p (k c j) -> p k c j", k=6, c=8, j=3)
        for j in range(3):
            STT(AB1a[:, j, :], modTv[:, 1, :, j], 1.0, gcols[:, 0:8], ALU.add, ALU.mult, [BmodT, Bgc], [Bab])
            CP("dve", AB1b[:, j, :], modTv[:, 0, :, j], [BmodT], [Bab])
        for j in range(NB):
            STT(AB2a[:, j, :], modTv[:, 4, :, j], 1.0, gcols[:, 8:16], ALU.add, ALU.mult, [BmodT, Bgc], [Bab])
            CP("dve", AB2b[:, j, :], modTv[:, 3, :, j], [BmodT], [Bab])
        k_ = 0
        for b in range(NB):
            for (kind, dst, goff) in ((2, GT1, 0), (5, GT2, 1024)):
                for half in range(2):
                    pb = PB[k_ % 2]
                    MM(pb[:, :], selsb[:, b * 128:(b + 1) * 128], modsb[:, kind * 1024 + half * 512: kind * 1024 + (half + 1) * 512],
                       True, True, [Bmod, Bsel], [Bp[k_ % 2]])
                    TTo("dve", dst[:, b, half * 512:(half + 1) * 512], pb[:, :], grow[:, goff + half * 512: goff + (half + 1) * 512],
                        ALU.mult, [Bp[k_ % 2], Bgrow], [Bgt])
                    k_ += 1
        CP("dve", gnw[:], grow[:, 2048:2560], [Bgrow], [Bgnw])
        ACT(lg[:], grow[:, 2560:2568], AF.Exp, [Bgrow], [Bdec], scale=-1.0)
        TS("dve", lg[:], lg[:], 1.0, None, ALU.add, None, [Bdec], [Bdec])
        ACT(lg[:], lg[:], AF.Ln, [Bdec], [Bdec])
        TS("dve", lg[:], lg[:], -1.0, None, ALU.mult, None, [Bdec], [Bdec])
        Af = cst[:, 128:256]
        Ab = cst[:, 256:384]
        R1 = cst[:, 384:512]
        R2 = cst[:, 512:640]
        C1 = cst[:, 640:641]
        C2 = cst[:, 641:642]
        tmpd = A.alloc([128, 128], F32, "tmpd")
        rs = 128.0 ** -0.5
        for h in range(4):
            ACT(DT[:, h, :], Af, AF.Exp, [Bcst, Bdec], [Bdec], scale=lg[:, h:h + 1])
            ACT(tmpd[:], Ab, AF.Exp, [Bcst, Bdec], [Bdec], scale=lg[:, 4 + h:5 + h])
            TTo("dve", DT[:, h, :], DT[:, h, :], tmpd[:], ALU.add, [Bdec], [Bdec])
            TS("dve", DT[:, h, :], DT[:, h, :], rs, None, ALU.mult, None, [Bdec], [Bdec])
            ACT(WF[:, h, :], R1, AF.Exp, [Bcst, Bdec], [Bdec], scale=lg[:, h:h + 1])
            TS("dve", WF[:, h, :], WF[:, h, :], rs, None, ALU.mult, None, [Bdec], [Bdec])
            ACT(WB[:, h, :], R2, AF.Exp, [Bcst, Bdec], [Bdec], scale=lg[:, 4 + h:5 + h])
            TS("dve", WB[:, h, :], WB[:, h, :], rs, None, ALU.mult, None, [Bdec], [Bdec])
            ACT(KW[:, h:h + 1], C1, AF.Exp, [Bcst, Bdec], [Bdec], scale=lg[:, h:h + 1])
            ACT(KW[:, 4 + h:5 + h], C2, AF.Exp, [Bcst, Bdec], [Bdec], scale=lg[:, 4 + h:5 + h])
        ACT(GAM[:], lg[:], AF.Exp, [Bdec], [Bdec], scale=128.0)
        S.barrier()
        A.reset(m_A)

        By = [[Buf(f"y{b}_{n}") for n in range(NT)] for b in range(NB)]
        Bh2 = [[Buf(f"h2{b}_{n}") for n in range(NT)] for b in range(NB)]

        for b in range(NB):
            A.reset(m_A)
            hT = A.alloc([128, 8, TT * 128], BF16, "hT")
            BhT = bufs(TT, "hT")
            mix = A.alloc([128, NT, 1024], BF16, "mix")
            Bmix = bufs(NT, "mix")
            wg = [A.alloc([128, 8, 512], BF16, "wg0"), A.alloc([128, 8, 512], BF16, "wg1")]
            Bwg = bufs(2, "wg")
            xt = [A.alloc([128, 1024], F32, "xt0"), A.alloc([128, 1024], F32, "xt1")]
            Bxt = bufs(2, "xt")
            xn = [A.alloc([128, 1024], BF16, "xn0"), A.alloc([128, 1024], BF16, "xn1")]
            Bxn = bufs(2, "xn")
            junk = A.alloc([128, 1024], BF16, "junk")
            Bjunk = Buf("junk")
            m_Ab = A.mark()

            for ti in range(TT):
                s_ = ti % 2
                src = ctx[b, ti * 128:(ti + 1) * 128, :] if ti < 2 else x[b, (ti - 2) * 128:(ti - 1) * 128, :]
                jm = 2 if ti < 2 else b
                DMA("sp", f"dx{s_}", xt[s_][:], src, [], [Bxt[s_]])
                ss = stat[:, s_ * 4:s_ * 4 + 1]
                ms = stat[:, s_ * 4 + 1:s_ * 4 + 2]
                rsd = stat[:, s_ * 4 + 2:s_ * 4 + 3]
                Bst = Bxn[s_]
                ACT(junk[:], xt[s_][:], AF.Square, [Bxt[s_]], [Bjunk, Bst], accum=ss)
                TS("dve", ms, ss, 1.0 / D, EPS, ALU.mult, ALU.add, [Bst], [Bst])
                rstd_from(ms, rsd, Bst, Bst)
                TS("dve", xn[s_][:], xt[s_][:], rsd, None, ALU.mult, None, [Bxt[s_], Bst], [Bst])
                pbi = 2 + s_
                ptv = PBb[pbi].rearrange("p (k t) -> p k t", k=8)
                for kc in range(8):
                    TR(ptv[:, kc, :], xn[s_][:, kc * 128:(kc + 1) * 128], [Bst], [Bp[pbi]], inc=(kc == 7))
                for kc in range(8):
                    o_ = hT[:, kc, ti * 128:(ti + 1) * 128]
                    if kc % 2 == 0:
                        ACT(o_, ptv[:, kc, :], AF.Identity, [Bp[pbi], Bab], [BhT[ti]],
                            scale=AB1a[:, jm, kc:kc + 1], bias=AB1b[:, jm, kc:kc + 1])
                    else:
                        TS("dve", o_, ptv[:, kc, :], AB1a[:, jm, kc:kc + 1], AB1b[:, jm, kc:kc + 1], ALU.mult, ALU.add,
                           [Bp[pbi], Bab], [BhT[ti]])

            qT = A.alloc([128, N], BF16, "qT")
            qfT = A.alloc([128, N], BF16, "qfT")
            qbT = A.alloc([128, N], BF16, "qbT")
            kT = A.alloc([128, N], BF16, "kT")
            kf = A.alloc([128, TT, 128], BF16, "kf")
            kb = A.alloc([128, TT, 128], BF16, "kb")
            vv = A.alloc([128, TT, 128], BF16, "vv")
            G = A.alloc([128, NT, 128], BF16, "G")
            S16f = A.alloc([128, NT, 128], BF16, "S16f")
            S16b = A.alloc([128, NT, 128], BF16, "S16b")
            qkr = [A.alloc([128, 256], BF16, "qkr0"), A.alloc([128, 256], BF16, "qkr1")]
            PTm = [A.alloc([128, 128], BF16, "PTm0"), A.alloc([128, 128], BF16, "PTm1")]
            pst = [A.alloc([128, 512], F32, "pst0"), A.alloc([128, 512], F32, "pst1")]
            t1 = [A.alloc([128, 256], F32, "t1_0"), A.alloc([128, 256], F32, "t1_1")]
            t2 = [A.alloc([128, 256], F32, "t2_0"), A.alloc([128, 256], F32, "t2_1")]
            rsil = [A.alloc([128, 128], F32, "rsil0"), A.alloc([128, 128], F32, "rsil1")]
            Sf = A.alloc([128, 128], F32, "Sf")
            Sb = A.alloc([128, 128], F32, "Sb")
            yln = [A.alloc([128, 128], F32, "yln0"), A.alloc([128, 128], F32, "yln1")]
            lnst = [A.alloc([128, 16], F32, "lnst0"), A.alloc([128, 16], F32, "lnst1")]
            Bq = bufs(NT, "q")
            Bk = bufs(TT, "kfb")
            BG = bufs(NT, "G")
            BS16f = bufs(NT, "S16f")
            BS16b = bufs(NT, "S16b")
            Bqkr = bufs(2, "qkr")
            BPTm = bufs(2, "PTm")
            Bpst = bufs(2, "pst")
            Bt12 = bufs(2, "t12")
            Brs = bufs(2, "rsil")
            BSf = Buf("Sf")
            BSb = Buf("Sb")
            Byln = bufs(2, "yln")

            cols0 = [0, 512, 1024, 1536]
            for h in range(4):
                ws = h % 2
                for j in range(4):
                    DMA("pool", f"dwg{ws}", wg[ws][:, :, j * 128:(j + 1) * 128],
                        w_in_v[:, :, cols0[j] + h * 128: cols0[j] + (h + 1) * 128], [], [Bwg[ws]])
                for ti in range(TT):
                    s_ = ti % 2
                    pb = PB[s_]
                    if ti < 2:
                        for kc in range(8):
                            MM(pb[:, 0:256], hT[:, kc, ti * 128:(ti + 1) * 128], wg[ws][:, kc, 128:384], kc == 0, kc == 7,
                               [BhT[ti], Bwg[ws]], [Bp[s_]], inc=(kc == 7))
                        TS("dve", kf[:, 16 + ti, :], pb[:, 0:128], KW[:, h:h + 1], None, ALU.mult, None, [Bp[s_], Bdec], [Bk[16 + ti]])
                        ACT(kb[:, 16 + ti, :], pb[:, 0:128], AF.Identity, [Bp[s_], Bdec], [Bk[16 + ti]], scale=KW[:, 4 + h:5 + h])
                        CP("dve", vv[:, 16 + ti, :], pb[:, 128:256], [Bp[s_]], [Bk[16 + ti]])
                        continue
                    n = ti - 2
                    for kc in range(8):
                        MM(pb[:, :], hT[:, kc, ti * 128:(ti + 1) * 128], wg[ws][:, kc, :], kc == 0, kc == 7,
                           [BhT[ti], Bwg[ws]], [Bp[s_]], inc=(kc == 7))
                    CP("act", pst[s_][:], pb[:, :], [Bp[s_]], [Bpst[s_]])
                    cosn = ropeT[:, 0, n, :]
                    sinn = ropeT[:, 1, n, :].rearrange("p (s h d) -> p s h d", s=2, h=2, d=32)
                    for qk in range(2):
                        TTo("pool", t1[s_][:, qk * 128:(qk + 1) * 128], pst[s_][:, qk * 128:(qk + 1) * 128], cosn, ALU.mult,
                            [Bpst[s_], Brope], [Bt12[s_]])
                        xv = pst[s_][:, qk * 128:(qk + 1) * 128].rearrange("p (s h d) -> p s h d", s=2, h=2, d=32)
                        tv = t2[s_][:, qk * 128:(qk + 1) * 128].rearrange("p (s h d) -> p s h d", s=2, h=2, d=32)
                        TTo("dve", tv[:, :, 0, :], xv[:, :, 1, :], sinn[:, :, 0, :], ALU.mult, [Bpst[s_], Brope], [Bt12[s_]])
                        TTo("dve", tv[:, :, 1, :], xv[:, :, 0, :], sinn[:, :, 1, :], ALU.mult, [Bpst[s_], Brope], [Bt12[s_]])
                    TTo("dve", qkr[s_][:], t1[s_][:], t2[s_][:], ALU.add, [Bt12[s_]], [Bqkr[s_]])
                    pbi = 2 + s_
                    ptv = PBb[pbi].rearrange("p (k t) -> p k t", k=8)
                    TR(ptv[:, 0, :], qkr[s_][:, 0:128], [Bqkr[s_]], [Bp[pbi]], inc=False)
                    TR(ptv[:, 1, :], qkr[s_][:, 128:256], [Bqkr[s_]], [Bp[pbi]])
                    cs = slice(n * 128, (n + 1) * 128)
                    CP("act", qT[:, cs], ptv[:, 0, :], [Bp[pbi]], [Bq[n]])
                    TTo("dve", qfT[:, cs], ptv[:, 0, :], WF[:, h, :], ALU.mult, [Bp[pbi], Bdec], [Bq[n]])
                    TTo("dve", qbT[:, cs], ptv[:, 0, :], WB[:, h, :], ALU.mult, [Bp[pbi], Bdec], [Bq[n]])
                    CP("act", kT[:, cs], ptv[:, 1, :], [Bp[pbi]], [Bq[n]])
                    TS("pool", kf[:, n, :], qkr[s_][:, 128:256], KW[:, h:h + 1], 1.0, ALU.mult, ALU.mult, [Bqkr[s_], Bdec], [Bk[n]])
                    TS("pool", kb[:, n, :], qkr[s_][:, 128:256], KW[:, 4 + h:5 + h], 1.0, ALU.mult, ALU.mult, [Bqkr[s_], Bdec], [Bk[n]])
                    CP("pool", vv[:, n, :], pst[s_][:, 256:384], [Bpst[s_]], [Bk[n]])
                    ACT(rsil[s_][:], pst[s_][:, 384:512], AF.Silu, [Bpst[s_]], [Brs[s_]])
                    TTo("pool", G[:, n, :], rsil[s_][:], gnw[:, h * 128:(h + 1) * 128], ALU.mult, [Brs[s_], Bgnw], [BG[n]])

                uslots = [(4, 0), (5, 0)]
                uc = [0]

                def U(kk, tile_i):
                    bi, si = uslots[uc[0] % 2]
                    uc[0] += 1
                    o_ = PB[bi][:, si * 128:(si + 1) * 128]
                    MM(o_, kk[:, tile_i, :], vv[:, tile_i, :], True, True, [Bk[tile_i]], [Bslot[bi][si]])
                    return o_, Bslot[bi][si]

                gf = GAM[:, h:h + 1]
                gb = GAM[:, 4 + h:5 + h]
                u0, bu0 = U(kf, 16)
                CP("dve", Sf[:], u0, [bu0], [BSf])
                u1, bu1 = U(kf, 17)
                STT(Sf[:], Sf[:], gf, u1, ALU.mult, ALU.add, [BSf, bu1, Bdec], [BSf])
                u0, bu0 = U(kb, 17)
                CP("dve", Sb[:], u0, [bu0], [BSb])
                u1, bu1 = U(kb, 16)
                STT(Sb[:], Sb[:], gb, u1, ALU.mult, ALU.add, [BSb, bu1, Bdec], [BSb])
                for n in range(NT):
                    nb_ = NT - 1 - n
                    CP("act", S16f[:, n, :], Sf[:], [BSf], [BS16f[n]])
                    CP("act", S16b[:, nb_, :], Sb[:], [BSb], [BS16b[nb_]])
                    if n < NT - 1:
                        u0, bu0 = U(kf, n)
                        STT(Sf[:], Sf[:], gf, u0, ALU.mult, ALU.add, [BSf, bu0, Bdec], [BSf])
                        u1, bu1 = U(kb, nb_)
                        STT(Sb[:], Sb[:], gb, u1, ALU.mult, ALU.add, [BSb, bu1, Bdec], [BSb])

                def scores(n):
                    cs = slice(n * 128, (n + 1) * 128)
                    MM(PB[4 + n % 2][:, 0:128], kT[:, cs], qT[:, cs], True, True, [Bq[n]], [Bslot[4 + n % 2][0]])

                scores(0)
                for n in range(NT):
                    if n + 1 < NT:
                        scores(n + 1)
                    s_ = n % 2
                    cs = slice(n * 128, (n + 1) * 128)
                    TTo("dve", PTm[s_][:], PB[4 + s_][:, 0:128], DT[:, h, :], ALU.mult,
                        [Bslot[4 + s_][0], Bdec], [BPTm[s_]])
                    o_ = PB[6 + s_][:, 0:128]
                    bo_ = Bslot[6 + s_][0]
                    MM(o_, PTm[s_][:], vv[:, n, :], True, False, [BPTm[s_], Bk[n]], [bo_], inc=False)
                    MM(o_, qfT[:, cs], S16f[:, n, :], False, False, [Bq[n], BS16f[n]], [bo_], inc=False)
                    MM(o_, qbT[:, cs], S16b[:, n, :], False, True, [Bq[n], BS16b[n]], [bo_])
                    ls = lnst[s_]
                    S.op("dve", lambda e, o_=o_, ls=ls: e.bn_stats(out=ls[:, 0:6], in_=o_), [bo_], [Byln[s_]])
                    S.op("dve", lambda e, ls=ls: e.bn_aggr(out=ls[:, 6:8], in_=ls[:, 0:6]), [Byln[s_]], [Byln[s_]])
                    TS("dve", ls[:, 8:9], ls[:, 7:8], EPS, None, ALU.add, None, [Byln[s_]], [Byln[s_]])
                    rstd_from(ls[:, 8:9], ls[:, 9:10], Byln[s_], Byln[s_])
                    TS("dve", yln[s_][:], o_, ls[:, 6:7], ls[:, 9:10], ALU.subtract, ALU.mult, [bo_, Byln[s_]], [Byln[s_]])
                    TTo("pool", mix[:, n, h * 128:(h + 1) * 128], yln[s_][:], G[:, n, :], ALU.mult, [Byln[s_], BG[n]], [Bmix[n]])

            S.barrier()
            A.reset(m_Ab)
            vaug = A.alloc([128, TT, 8, 65], BF16, "vaug")
            Bva = bufs(TT, "vaug")
            wqk = [A.alloc([128, 8, 256], BF16, "wqk0"), A.alloc([128, 8, 256], BF16, "wqk1")]
            Bwqk = bufs(2, "wqk")
            qTp = [A.alloc([128, N], BF16, "qTp0"), A.alloc([128, N], BF16, "qTp1")]
            BqTp = Buf("qTp")
            kTp = A.alloc([128, TT * 128], BF16, "kTp")
            BkTp = Buf("kTp")
            PT = [A.alloc([128, 896], BF16, "PT0"), A.alloc([128, 896], BF16, "PT1")]
            BPT = bufs(2, "PT")
            sc = [A.alloc([128, 640], F32, "sc0"), A.alloc([128, 640], F32, "sc1")]
            Bsc = bufs(2, "sc")
            nabT = [A.alloc([128, 1536], F32, "nab0"), A.alloc([128, 1536], F32, "nab1")]
            Bnab = bufs(2, "nab")
            rcp = A.alloc([128, 8], F32, "rcp")
            Brcp = bufs(2, "rcp")

            S.op("pool", lambda e, v_=vaug: e.memset(v_[:, :, :, 64:65], 1.0), [], Bva)
            S.op("pool", lambda e, q_=qTp[0]: e.memset(q_[64:128, :], 0.0), [], [BqTp])
            S.op("pool", lambda e, q_=qTp[1]: e.memset(q_[0:64, :], 0.0), [], [BqTp])
            DMA("pool", "dwg0", wg[0][:], w_in_v[:, :, 3072:3584], [], [Bwg[0]])
            for ti in range(TT):
                s_ = ti % 2
                pb = PB[s_]
                for kc in range(8):
                    MM(pb[:, :], hT[:, kc, ti * 128:(ti + 1) * 128], wg[0][:, kc, :], kc == 0, kc == 7,
                       [BhT[ti], Bwg[0]], [Bp[s_]], inc=(kc == 7))
                CP("act" if ti % 2 == 0 else "dve", vaug[:, ti, :, 0:64], pb[:, :].rearrange("p (h d) -> p h d", h=8),
                   [Bp[s_]], [Bva[ti]])
            for pr in range(4):
                ws = pr % 2
                DMA("pool", f"dwq{ws}", wqk[ws][:, :, 0:128], w_in_v[:, :, 2048 + pr * 128: 2048 + (pr + 1) * 128], [], [Bwqk[ws]])
                DMA("pool", f"dwq{ws}", wqk[ws][:, :, 128:256], w_in_v[:, :, 2560 + pr * 128: 2560 + (pr + 1) * 128], [], [Bwqk[ws]])
                for tg in range(4):
                    s_ = tg % 2
                    pb = PB[s_]
                    for kc in range(8):
                        MM(pb[:, :], wqk[ws][:, kc, 0:128], hT[:, kc, 256 + tg * 512: 256 + (tg + 1) * 512], kc == 0, kc == 7,
                           BhT[2 + tg * 4: 6 + tg * 4] + [Bwqk[ws]], [Bp[s_]], inc=(kc == 7))
                    ACT(qTp[0][0:64, tg * 512:(tg + 1) * 512], pb[0:64, :], AF.Copy, [Bp[s_]], [BqTp], scale=0.125)
                    TS("dve", qTp[1][64:128, tg * 512:(tg + 1) * 512], pb[64:128, :], 0.125, None, ALU.mult, None, [Bp[s_]], [BqTp])
                for tg in range(5):
                    s_ = tg % 2
                    pb = PB[s_]
                    c0, c1 = (0, 256) if tg == 0 else (256 + (tg - 1) * 512, 256 + tg * 512)
                    hb = BhT[0:2] if tg == 0 else BhT[2 + (tg - 1) * 4: 6 + (tg - 1) * 4]
                    for kc in range(8):
                        MM(pb[:, 0:c1 - c0], wqk[ws][:, kc, 128:256], hT[:, kc, c0:c1], kc == 0, kc == 7,
                           hb + [Bwqk[ws]], [Bp[s_]], inc=(kc == 7))
                    CP("act" if tg % 2 == 0 else "dve", kTp[:, c0:c1], pb[:, 0:c1 - c0], [Bp[s_]], [BkTp])
                its = [(t, s2) for t in range(NT) for s2 in range(2)]

                def qk_stage(i):
                    t, s2 = its[i]
                    a = i % 2
                    ul, idx0 = na_key_tiles(t)
                    X, Y = PB[2 + a], PB[4 + a]
                    rhs = qTp[s2][:, t * 128:(t + 1) * 128]
                    for j, u in enumerate(ul):
                        o_ = X[:, j * 128:(j + 1) * 128] if j < 4 else Y[:, 0:128]
                        wb_ = [Bp[2 + a]] if j < 4 else Bslot[4 + a]
                        MM(o_, kTp[:, 256 + u * 128: 256 + (u + 1) * 128], rhs, True, True, [BkTp, BqTp], wb_, inc=False)
                    for m in range(2):
                        MM(Y[:, 128 + m * 128: 256 + m * 128], kTp[:, m * 128:(m + 1) * 128], rhs, True, True,
                           [BkTp, BqTp], Bslot[4 + a], inc=(m == 1))

                for s2 in range(2):
                    hh = 2 * pr + s2
                    DMA("sp", f"dnab{s2}", nabT[s2][:], nab_d[hh], [], [Bnab[s2]])
                qk_stage(0)
                for i in range(len(its)):
                    if i + 1 < len(its):
                        qk_stage(i + 1)
                    t, s2 = its[i]
                    hh = 2 * pr + s2
                    a = i % 2
                    ul, idx0 = na_key_tiles(t)
                    nl = len(ul)
                    X, Y = PB[2 + a], PB[4 + a]
                    Z = PB[6 + a]
                    TTo("dve", sc[a][:, 0:512], X[:, :], nabT[s2][:, idx0 * 128:(idx0 + 4) * 128], ALU.add,
                        [Bp[2 + a], Bnab[s2]], [Bsc[a]])
                    if nl == 5:
                        TTo("dve", sc[a][:, 512:640], Y[:, 0:128], nabT[s2][:, (idx0 + 4) * 128:(idx0 + 5) * 128], ALU.add,
                            Bslot[4 + a] + [Bnab[s2]], [Bsc[a]])
                    ACT(PT[a][:, 0:nl * 128], sc[a][:, 0:nl * 128], AF.Exp, [Bsc[a]], [BPT[a]])
                    ACT(PT[a][:, 640:896], Y[:, 128:384], AF.Exp, Bslot[4 + a], [BPT[a]])
                    nmm = nl + 2
                    k2 = 0
                    for j, u in enumerate(ul):
                        MM(Z[:, 0:65], PT[a][:, j * 128:(j + 1) * 128], vaug[:, 2 + u, hh, :], k2 == 0, False,
                           [BPT[a], Bva[2 + u]], Bslot[6 + a], inc=False)
                        k2 += 1
                    for m in range(2):
                        MM(Z[:, 0:65], PT[a][:, 640 + m * 128: 768 + m * 128], vaug[:, m, hh, :], False, m == 1,
                           [BPT[a], Bva[m]], Bslot[6 + a], inc=(m == 1))
                    S.op("dve", lambda e, Z=Z, a=a, r_=rcp: e.reciprocal(out=r_[:, a:a + 1], in_=Z[:, 64:65]), Bslot[6 + a], [Brcp[a]])
                    TS("dve", mix[:, t, 512 + hh * 64: 512 + (hh + 1) * 64], Z[:, 0:64], rcp[:, a:a + 1], None, ALU.mult, None,
                       Bslot[6 + a] + [Brcp[a]], [Bmix[t]])

            S.barrier()
            A.reset(m_Ab)
            woutb = A.alloc([128, 8, 1024], BF16, "wout")
            Bwout = Buf("wout")
            DMA("pool", "dwo", woutb[:, :, 0:512], w_out_v[:, :, 0:512], [], [Bwout])
            DMA("pool", "dwo", woutb[:, :, 512:1024], w_out_v[:, :, 512:1024], [], [Bwout])
            mixT = [A.alloc([128, 8, 128], BF16, "mixT0"), A.alloc([128, 8, 128], BF16, "mixT1")]
            BmixT = bufs(2, "mixT")
            tmp = [A.alloc([128, 1024], F32, "tmp0"), A.alloc([128, 1024], F32, "tmp1")]
            Btmp = bufs(2, "tmp")
            x1 = [A.alloc([128, 1024], F32, "x1_0"), A.alloc([128, 1024], F32, "x1_1")]
            Bx1 = bufs(2, "x1")
            h2t = [A.alloc([128, 8, 128], BF16, "h2t0"), A.alloc([128, 8, 128], BF16, "h2t1")]
            Bh2t = bufs(2, "h2t")
            st2 = A.alloc([128, 32], F32, "st2")
            Bst2 = bufs(2, "st2")
            for n in range(NT):
                s_ = n % 2
                DMA("sp", f"dx{s_}", xt[s_][:], x[b, n * 128:(n + 1) * 128, :], [], [Bxt[s_]])
                ptv = PBb[2].rearrange("p (k t) -> p k t", k=8)
                for kc in range(8):
                    TR(ptv[:, kc, :], mix[:, n, kc * 128:(kc + 1) * 128], [Bmix[n]], [Bp[2]], inc=(kc == 7))
                CP("act", mixT[s_][:].rearrange("p k t -> p (k t)"), PBb[2], [Bp[2]], [BmixT[s_]])
                sq = st2[:, s_ * 16:(s_ + 1) * 16]
                for half in range(2):
                    for kc in range(8):
                        MM(PB[half][:, :], mixT[s_][:, kc, :], woutb[:, kc, half * 512:(half + 1) * 512], kc == 0, kc == 7,
                           [BmixT[s_], Bwout], [Bp[half]], inc=(kc == 7))
                    ACT(junk[:, 0:512], PB[half][:, :], AF.Square, [Bp[half]], [Bjunk, Bst2[s_]], accum=sq[:, half:half + 1])
                TTo("dve", sq[:, 2:3], sq[:, 0:1], sq[:, 1:2], ALU.add, [Bst2[s_]], [Bst2[s_]])
                TS("dve", sq[:, 3:4], sq[:, 2:3], 1.0 / D, EPS, ALU.mult, ALU.add, [Bst2[s_]], [Bst2[s_]])
                rstd_from(sq[:, 3:4], sq[:, 4:5], Bst2[s_], Bst2[s_])
                for half in range(2):
                    STT(tmp[s_][:, half * 512:(half + 1) * 512], PB[half][:, :], sq[:, 4:5], GT1[:, b, half * 512:(half + 1) * 512],
                        ALU.mult, ALU.mult, [Bp[half], Bst2[s_], Bgt], [Btmp[s_]])
                TTo("pool", x1[s_][:], tmp[s_][:], xt[s_][:], ALU.add, [Btmp[s_], Bxt[s_]], [Bx1[s_]])
                DMA("sp", f"dst{s_}", y[b, n * 128:(n + 1) * 128, :], x1[s_][:], [Bx1[s_]], [By[b][n]])
                ACT(junk[:], x1[s_][:], AF.Square, [Bx1[s_]], [Bjunk, Bst2[s_]], accum=sq[:, 5:6])
                TS("dve", sq[:, 6:7], sq[:, 5:6], 1.0 / D, EPS, ALU.mult, ALU.add, [Bst2[s_]], [Bst2[s_]])
                rstd_from(sq[:, 6:7], sq[:, 7:8], Bst2[s_], Bst2[s_])
                TS("dve", xn[s_][:], x1[s_][:], sq[:, 7:8], None, ALU.mult, None, [Bx1[s_], Bst2[s_]], [Bxn[s_]])
                ptv3 = PBb[3].rearrange("p (k t) -> p k t", k=8)
                for kc in range(8):
                    TR(ptv3[:, kc, :], xn[s_][:, kc * 128:(kc + 1) * 128], [Bxn[s_]], [Bp[3]], inc=(kc == 7))
                for kc in range(8):
                    if kc % 2 == 0:
                        ACT(h2t[s_][:, kc, :], ptv3[:, kc, :], AF.Identity, [Bp[3], Bab], [Bh2t[s_]],
                            scale=AB2a[:, b, kc:kc + 1], bias=AB2b[:, b, kc:kc + 1])
                    else:
                        TS("dve", h2t[s_][:, kc, :], ptv3[:, kc, :], AB2a[:, b, kc:kc + 1], AB2b[:, b, kc:kc + 1], ALU.mult, ALU.add,
                           [Bp[3], Bab], [Bh2t[s_]])
                DMA("sp", f"dh{s_}", h2s[b][:, :, n * 128:(n + 1) * 128], h2t[s_][:], [Bh2t[s_]], [Bh2[b][n]])
            S.barrier()

        A.reset(m_global)
        w1b = A.alloc([128, 8, 4096], BF16, "w1b")
        Bw1 = bufs(4, "w1")
        w2b = A.alloc([128, 32, 1024], BF16, "w2b")
        Bw2 = bufs(4, "w2")
        for i in range(4):
            DMA("pool", "dw1", w1b[:, :, i * 1024:(i + 1) * 1024], w1_v[:, :, i * 1024:(i + 1) * 1024], [], [Bw1[i]])
        for i in range(4):
            DMA("pool", "dw2", w2b[:, i * 8:(i + 1) * 8, :], w2_v[:, i * 8:(i + 1) * 8, :], [], [Bw2[i]])
        for i in range(4):
            Bw1[i].w = ("dw1", S.cnt["dw1"])
            Bw2[i].w = ("dw2", S.cnt["dw2"])
        h2g = [A.alloc([128, 8, 256], BF16, "h2g0"), A.alloc([128, 8, 256], BF16, "h2g1")]
        Bh2g = bufs(2, "h2g")
        aT = A.alloc([128, 32, 256], BF16, "aT")
        BaT = bufs(32, "aT")
        rr = [A.alloc([128, 256], F32, "rr0"), A.alloc([128, 256], F32, "rr1")]
        Brr = bufs(2, "rr")
        x1l = [A.alloc([128, 1024], F32, f"x1l{i}") for i in range(2)]
        Bx1l = bufs(2, "x1l")
        tmp2 = [A.alloc([128, 1024], F32, f"tmpb{i}") for i in range(2)]
        Btmp2 = bufs(2, "tmp2")
        ob = [A.alloc([128, 1024], F32, f"ob{i}") for i in range(2)]
        Bob = bufs(2, "ob")
        junk2 = A.alloc([128, 512], BF16, "junk2")
        Bjunk2 = Buf("junk2")
        st3 = A.alloc([128, 32], F32, "st3")
        Bst3 = bufs(2, "st3")
        gi = 0
        for b in range(NB):
            for g in range(NT // 2):
                gs = gi % 2
                DMA("sp", f"dh{gs}", h2g[gs][:], h2s[b][:, :, g * 256:(g + 1) * 256], [Bh2[b][2 * g], Bh2[b][2 * g + 1]], [Bh2g[gs]])
                for f in range(32):
                    pm = PB[f % 2]
                    for kc in range(8):
                        MM(pm[:, 0:256], w1b[:, kc, f * 128:(f + 1) * 128], h2g[gs][:, kc, :], kc == 0, kc == 7,
                           [Bw1[f // 8], Bh2g[gs]], [Bp[f % 2]], inc=(kc == 7))
                    ACT(rr[f % 2][:], pm[:, 0:256], AF.Relu, [Bp[f % 2]], [Brr[f % 2]])
                    TTo("pool", aT[:, f, :], rr[f % 2][:], rr[f % 2][:], ALU.mult, [Brr[f % 2]], [BaT[f]])
                for tl in range(2):
                    n = 2 * g + tl
                    DMA("sp", f"dxl{tl}", x1l[tl][:], y[b, n * 128:(n + 1) * 128, :], [By[b][n]], [Bx1l[tl]])
                    sq = st3[:, tl * 16:(tl + 1) * 16]
                    for half in range(2):
                        bi = 4 + tl * 2 + half
                        for f in range(32):
                            MM(PB[bi][:, :], aT[:, f, tl * 128:(tl + 1) * 128], w2b[:, f, half * 512:(half + 1) * 512], f == 0, f == 31,
                               [BaT[f], Bw2[f // 8]], Bslot[bi], inc=(f == 31))
                        ACT(junk2[:], PB[bi][:, :], AF.Square, Bslot[bi], [Bjunk2, Bst3[tl]], accum=sq[:, half:half + 1])
                    TTo("dve", sq[:, 2:3], sq[:, 0:1], sq[:, 1:2], ALU.add, [Bst3[tl]], [Bst3[tl]])
                    TS("dve", sq[:, 3:4], sq[:, 2:3], 1.0 / D, EPS, ALU.mult, ALU.add, [Bst3[tl]], [Bst3[tl]])
                    rstd_from(sq[:, 3:4], sq[:, 4:5], Bst3[tl], Bst3[tl])
                    for half in range(2):
                        bi = 4 + tl * 2 + half
                        STT(tmp2[tl][:, half * 512:(half + 1) * 512], PB[bi][:, :], sq[:, 4:5], GT2[:, b, half * 512:(half + 1) * 512],
                            ALU.mult, ALU.mult, Bslot[bi] + [Bst3[tl], Bgt], [Btmp2[tl]])
                    TTo("pool", ob[tl][:], tmp2[tl][:], x1l[tl][:], ALU.add, [Btmp2[tl], Bx1l[tl]], [Bob[tl]])
                    DMA("sp", f"dst{tl}", y[b, n * 128:(n + 1) * 128, :], ob[tl][:], [Bob[tl], Bx1l[tl]], [By[b][n]])
                gi += 1
        S.final_wait("sp")
        S.emit()
    return nc


def _host_consts():
    i = np.arange(128)
    BIG = 1.0e5
    jj, ii = np.meshgrid(i, i, indexing="ij")
    A_f = np.where(ii >= jj, (ii - jj).astype(np.float32), BIG).astype(np.float32)
    A_b = np.where(jj > ii, (jj - ii).astype(np.float32), BIG).astype(np.float32)
    R1 = np.broadcast_to((i + 1).astype(np.float32)[None, :], (128, 128))
    R2 = np.broadcast_to((128 - i).astype(np.float32)[None, :], (128, 128))
    C1 = (127 - i).astype(np.float32)[:, None]
    C2 = i.astype(np.float32)[:, None]
    consts = np.concatenate([np.eye(128, dtype=np.float32), A_f, A_b, R1, R2, C1, C2], axis=1)
    inv = (np.float32(10000.0) ** (-np.arange(32, dtype=np.float32) / np.float32(32))).astype(np.float32)
    tok = np.arange(N)
    pos_r = (tok // 64).astype(np.float32)
    pos_c = (tok % 64).astype(np.float32)
    ar = (pos_r[:, None] * inv[None, :]).astype(np.float32)
    ac = (pos_c[:, None] * inv[None, :]).astype(np.float32)
    cos = np.concatenate([np.cos(ar), np.cos(ar), np.cos(ac), np.cos(ac)], axis=1).astype(np.float32)
    sins = np.concatenate([-np.sin(ar), np.sin(ar), -np.sin(ac), np.sin(ac)], axis=1).astype(np.float32)
    rope = np.stack([cos.reshape(16, 128, 128).transpose(1, 0, 2), sins.reshape(16, 128, 128).transpose(1, 0, 2)], axis=1)
    rope = np.ascontiguousarray(rope.reshape(128, 2 * 16 * 128))
    sel = np.zeros((3, 3 * 128 + 3), np.float32)
    for b in range(3):
        sel[b, b * 128:(b + 1) * 128] = 1.0
        sel[b, 384 + b] = 1.0
    return np.ascontiguousarray(consts), rope, sel


def _na_bias_tables(rpb):
    a = np.arange(2)[:, None, None, None]
    cp = np.arange(64)[None, :, None, None]
    bq = np.arange(2)[None, None, :, None]
    c = np.arange(64)[None, None, None, :]
    col_start = np.clip(c - 8, 0, 48)
    validc = (cp >= col_start) & (cp < col_start + 16)
    dc = np.clip(cp - c + 15, 0, 30)
    out = np.full((8, 2, 64, 12, 2, 64), NEG, np.float32)
    for idx in range(12):
        if idx < 5:
            delta = idx - 2
        else:
            delta = idx - 5 - 3
        dr = 2 * delta + a - bq
        if idx < 5:
            validr = (dr >= -4) & (dr <= 3)
        else:
            validr = (dr >= -7) & (dr <= 7)
        valid = np.broadcast_to(validr & validc, (2, 64, 2, 64))
        dri = np.broadcast_to(np.clip(dr + 7, 0, 14), (2, 64, 2, 64))
        dci = np.broadcast_to(dc, (2, 64, 2, 64))
        g = rpb[:, dri, dci]
        out[:, :, :, idx, :, :] = np.where(valid[None], g, np.float32(NEG))
    return np.ascontiguousarray(out.reshape(8, 128, 12 * 128))


def _prep(inputs, NB, n_cores):
    f = lambda a: np.ascontiguousarray(np.asarray(a, dtype=np.float32))
    x = f(inputs["x"]); c = f(inputs["c"]); ctx = f(inputs["ctx"]); c_ctx = f(inputs["c_ctx"])
    consts, rope, sel = _host_consts()
    gcols = np.concatenate([f(inputs["g_pre_mix"])[0].reshape(8, 128).T, f(inputs["g_pre_mlp"])[0].reshape(8, 128).T], axis=1)
    growv = np.concatenate([f(inputs["g_post_mix"])[0], f(inputs["g_post_mlp"])[0], f(inputs["ret_gn"])[0],
                            f(inputs["ret_decay"])[0].reshape(8)])
    grow = np.ascontiguousarray(np.broadcast_to(growv[None, :], (128, growv.shape[0])))
    b_ada3 = np.ascontiguousarray(np.broadcast_to(f(inputs["b_ada"])[0][None, :], (3, 6144)))
    nab = _na_bias_tables(f(inputs["na_rpb"])[0])
    shared = {
        "w_ada": f(inputs["w_ada"])[0], "b_ada3": b_ada3, "w_in": f(inputs["w_in"])[0], "w_out": f(inputs["w_out"])[0],
        "w1": f(inputs["w_mlp1"])[0], "w2": f(inputs["w_mlp2"])[0], "gcols": np.ascontiguousarray(gcols), "grow": grow,
        "consts": consts, "rope": rope, "nab": nab, "sel": sel,
    }
    in_maps = []
    for i in range(n_cores):
        rows = [c[i * NB + j] for j in range(NB)]
        while len(rows) < 2:
            rows.append(rows[-1])
        cc = np.stack(rows + [c_ctx], axis=0)
        cT = np.ascontiguousarray(cc.reshape(3, 8, 128).transpose(2, 1, 0).reshape(128, 24))
        m = dict(shared)
        m["x"] = np.ascontiguousarray(x[i * NB:(i + 1) * NB])
        m["ctx"] = np.ascontiguousarray(ctx[i * NB:(i + 1) * NB])
        m["cT"] = cT
        in_maps.append(m)
    return in_maps


def kernel(**inputs):
    n_cores = 8
    NB = 2
    in_maps = _prep(inputs, NB, n_cores)
    nc = build(NB)
    res = run_bass_kernel_spmd(nc, in_maps, core_ids=list(range(n_cores)))
    out = np.concatenate([np.asarray(r["y"], dtype=np.float32) for r in res.results], axis=0)
    return out
```
